# Optimizing a Trainium2 kernel written in Bass

```python
import math
import jax
import jax.numpy as jnp
from jax import lax
import numpy as np

D_MODEL = 1024
BATCH = 2
SEQ = 16384
DEPTH = 2
DEC_BATCH = 16
DEC_SEQ = 32
PAST_LEN = 1024

CHUNK = 64
D_FF = 2816
CONV_CH = 512
CONV_WIDTH = 31
SSD_HEADS = 16
SSD_HEAD_DIM = 64
SSD_INNER = SSD_HEADS * SSD_HEAD_DIM
SSD_GROUPS = 2
SSD_STATE = 128
SSD_CONV_WIDTH = 4
SSD_CONV_CH = SSD_INNER + 2 * SSD_GROUPS * SSD_STATE
SSD_CHUNK = CHUNK
ATT_HEADS = 4
ATT_QK_DIM = 64
ATT_V_DIM = 2 * ATT_QK_DIM
ATT_WIDTH = ATT_HEADS * ATT_V_DIM
ATT_Q_BLOCK = 128
ATT_SCALE = ATT_QK_DIM ** -0.5
ALIBI_SLOPES = tuple(2.0 ** (-8.0 * (h + 1) / ATT_HEADS) for h in range(ATT_HEADS))
N_BRANCH = 3
N_SUB = 3
EPS = 1e-6
IN_SPLITS = (2 * CONV_CH, SSD_INNER, SSD_CONV_CH, SSD_HEADS,
             2 * ATT_HEADS * ATT_QK_DIM, 2 * ATT_HEADS * ATT_QK_DIM, ATT_WIDTH,
             N_BRANCH * D_MODEL)
N_IN = sum(IN_SPLITS)

kernel_name = 'hybrid_stream_conv_ssd_diffattn'


def _split_points():
    pts, acc = [], 0
    for s in IN_SPLITS[:-1]:
        acc += s
        pts.append(acc)
    return pts


def rms_norm(x, g, eps=EPS):
    xf = x.astype(jnp.float32)
    y = xf * lax.rsqrt(jnp.mean(xf * xf, axis=-1, keepdims=True) + eps)
    return (y * g.astype(jnp.float32)).astype(x.dtype)


def layer_norm(x, g, b, eps=1e-5):
    xf = x.astype(jnp.float32)
    mu = jnp.mean(xf, axis=-1, keepdims=True)
    var = jnp.mean(jnp.square(xf - mu), axis=-1, keepdims=True)
    y = (xf - mu) * lax.rsqrt(var + eps)
    return (y * g.astype(jnp.float32) + b.astype(jnp.float32)).astype(x.dtype)


def gated_group_rms(y, z, g):
    b, L, dn = y.shape
    yz = (y.astype(jnp.float32) * jax.nn.silu(z.astype(jnp.float32))).reshape(b, L, SSD_GROUPS, dn // SSD_GROUPS)
    yz = yz * lax.rsqrt(jnp.mean(yz * yz, axis=-1, keepdims=True) + EPS)
    return (yz.reshape(b, L, dn) * g.astype(jnp.float32)).astype(y.dtype)


def modulate(x, g, shift, scale):
    return rms_norm(x, g) * (1.0 + scale[:, None, :]) + shift[:, None, :]


def swiglu(h, w_in, w_out):
    up, gt = jnp.split(h @ w_in, 2, axis=-1)
    return (jax.nn.silu(gt) * up) @ w_out


def causal_dwconv(x, buf, w, b):
    width = w.shape[0]
    xp = jnp.concatenate([buf.astype(x.dtype), x], axis=1)
    y = lax.conv_general_dilated(xp, w[:, None, :].astype(x.dtype), (1,), 'VALID',
                                 dimension_numbers=('NWC', 'WIO', 'NWC'),
                                 feature_group_count=x.shape[-1])
    return y + b, xp[:, xp.shape[1] - (width - 1):]


def ssd_scan(x, dt, A, B, C, h0):
    bsz, L, H, P = x.shape
    G, N = B.shape[2], B.shape[3]
    R = H // G
    Q = SSD_CHUNK if L % SSD_CHUNK == 0 else L
    nc = L // Q
    f32 = jnp.float32
    xdt = (x.astype(f32) * dt[..., None]).reshape(bsz, nc, Q, G, R, P)
    a_cum = jnp.cumsum((dt * A).reshape(bsz, nc, Q, G, R), axis=2)
    Bc = B.astype(f32).reshape(bsz, nc, Q, G, N)
    Cc = C.astype(f32).reshape(bsz, nc, Q, G, N)
    causal = jnp.tril(jnp.ones((Q, Q), dtype=bool))[None, None, :, :, None, None]
    seg = a_cum[:, :, :, None] - a_cum[:, :, None, :]
    decay_ls = jnp.exp(jnp.where(causal, seg, -jnp.inf))
    cb = jnp.einsum('bclgn,bcsgn->bclsg', Cc, Bc)
    y_diag = jnp.einsum('bclsg,bclsgr,bcsgrp->bclgrp', cb, decay_ls, xdt)
    decay_to_end = jnp.exp(a_cum[:, :, -1:] - a_cum)
    chunk_states = jnp.einsum('bcsgn,bcsgr,bcsgrp->bcgrpn', Bc, decay_to_end, xdt)
    chunk_decay = jnp.exp(a_cum[:, :, -1])

    def step(h, inp):
        s_c, d_c = inp
        return h * d_c[..., None, None] + s_c, h

    h_last, h_prev = lax.scan(step, h0.astype(f32).reshape(bsz, G, R, P, N),
                              (jnp.moveaxis(chunk_states, 1, 0), jnp.moveaxis(chunk_decay, 1, 0)))
    h_prev = jnp.moveaxis(h_prev, 0, 1)
    y_off = jnp.einsum('bclgn,bcgrpn,bclgr->bclgrp', Cc, h_prev, jnp.exp(a_cum))
    y = (y_diag + y_off).reshape(bsz, L, H, P)
    return y.astype(x.dtype), h_last.reshape(bsz, H, P, N).astype(h0.dtype)


def diff_attend(q, k, v, qpos, kpos, lam):
    s = jnp.einsum('bqhcd,bkhcd->bhcqk', q, k).astype(jnp.float32) * ATT_SCALE
    dist = jnp.abs(qpos[:, None] - kpos[None, :]).astype(jnp.float32)
    slopes = jnp.asarray(ALIBI_SLOPES, dtype=jnp.float32)
    s = s - slopes[None, :, None, None, None] * dist[None, None, None]
    visible = (kpos[None, :] // CHUNK) <= (qpos[:, None] // CHUNK)
    p = jax.nn.softmax(jnp.where(visible, s, -jnp.inf), axis=-1)
    a = p[:, :, 0] - lam * p[:, :, 1]
    return jnp.einsum('bhqk,bkhe->bqhe', a.astype(v.dtype), v)


def prompt_attend(q, k, v, lam):
    b, S = q.shape[0], q.shape[1]
    nb = S // ATT_Q_BLOCK
    qb = jnp.moveaxis(q.reshape(b, nb, ATT_Q_BLOCK, ATT_HEADS, 2, ATT_QK_DIM), 1, 0)
    kpos = jnp.arange(S, dtype=jnp.int32)

    def one(args):
        q_blk, start = args
        qpos = start + jnp.arange(ATT_Q_BLOCK, dtype=jnp.int32)
        return diff_attend(q_blk, k, v, qpos, kpos, lam)

    o = lax.map(one, (qb, jnp.arange(nb, dtype=jnp.int32) * ATT_Q_BLOCK))
    return jnp.moveaxis(o, 0, 1).reshape(b, S, ATT_HEADS, ATT_V_DIM)


def make_sample_attend(cache_k, cache_v):
    def attend(q, k, v, lam):
        b, Ls = q.shape[0], q.shape[1]
        P = cache_k.shape[1]
        kk = jnp.concatenate([cache_k.reshape(b, P, ATT_HEADS, 2, ATT_QK_DIM).astype(k.dtype), k], axis=1)
        vv = jnp.concatenate([cache_v.astype(v.dtype), v], axis=1)
        qpos = P + jnp.arange(Ls, dtype=jnp.int32)
        kpos = jnp.arange(P + Ls, dtype=jnp.int32)
        return diff_attend(q, kk, vv, qpos, kpos, lam)
    return attend


def lambda_init(layer):
    return 0.8 - 0.6 * math.exp(-0.3 * layer)


def token_mix(h, p, layer, conv_buf, ssd_conv_buf, ssd_h0, attend):
    b, L, _ = h.shape
    u = h @ p['w_in']
    glu, z, xbc, dt_raw, q, k, v, gate_logits = jnp.split(u, _split_points(), axis=-1)
    ga, gg = jnp.split(glu, 2, axis=-1)
    a = ga * jax.nn.sigmoid(gg)
    a, conv_new = causal_dwconv(a, conv_buf, p['conv_dw_w'], p['conv_dw_b'])
    a_out = jax.nn.silu(layer_norm(a, p['conv_ln_g'], p['conv_ln_b'])) @ p['w_br_conv']
    xbc, ssd_conv_new = causal_dwconv(xbc, ssd_conv_buf, p['ssd_conv_w'], p['ssd_conv_b'])
    xbc = jax.nn.silu(xbc)
    xs, Bm, Cm = jnp.split(xbc, [SSD_INNER, SSD_INNER + SSD_GROUPS * SSD_STATE], axis=-1)
    dt = jax.nn.softplus((dt_raw + p['ssd_dt_bias']).astype(jnp.float32))
    A = -jnp.exp(p['ssd_A_log'].astype(jnp.float32))
    xs_h = xs.reshape(b, L, SSD_HEADS, SSD_HEAD_DIM)
    y, h_new = ssd_scan(xs_h, dt, A, Bm.reshape(b, L, SSD_GROUPS, SSD_STATE),
                        Cm.reshape(b, L, SSD_GROUPS, SSD_STATE), ssd_h0)
    y = y + p['ssd_D'][:, None] * xs_h
    y = gated_group_rms(y.reshape(b, L, SSD_INNER), z, p['ssd_norm_g'])
    b_out = y @ p['w_br_ssd']
    q = q.reshape(b, L, ATT_HEADS, 2, ATT_QK_DIM)
    k = k.reshape(b, L, ATT_HEADS, 2, ATT_QK_DIM)
    v = v.reshape(b, L, ATT_HEADS, ATT_V_DIM)
    lq = p['lambda_q'].astype(jnp.float32)
    lk = p['lambda_k'].astype(jnp.float32)
    lam0 = lambda_init(layer)
    lam = jnp.exp(jnp.sum(lq[0] * lk[0])) - jnp.exp(jnp.sum(lq[1] * lk[1])) + lam0
    o = attend(q, k, v, lam)
    o = rms_norm(o, p['attn_subln_g'], 1e-5) * (1.0 - lam0)
    c_out = o.reshape(b, L, ATT_WIDTH) @ p['w_br_attn']
    gates = jax.nn.sigmoid(gate_logits.reshape(b, L, N_BRANCH, D_MODEL) + p['b_gate'])
    merged = gates[:, :, 0] * a_out + gates[:, :, 1] * b_out + gates[:, :, 2] * c_out
    out = merged @ p['w_mix_out']
    k_rows = k.reshape(b, L, ATT_HEADS, 2 * ATT_QK_DIM)
    return out, (k_rows, v, conv_new, ssd_conv_new, h_new)


def trunk_layer(x, c, p, layer, conv_buf, ssd_conv_buf, ssd_h0, attend):
    mod = (jax.nn.silu(c) @ p['w_ada'] + p['b_ada']).reshape(c.shape[0], N_SUB, 3, D_MODEL)
    shift, scale, gate = mod[:, :, 0], mod[:, :, 1], mod[:, :, 2]
    h = modulate(x, p['norm_pre'][0], shift[:, 0], scale[:, 0])
    f = swiglu(h, p['w_ffn_in'][0], p['w_ffn_out'][0])
    x = x + 0.5 * gate[:, 0, None, :] * rms_norm(f, p['norm_post'][0])
    h = modulate(x, p['norm_pre'][1], shift[:, 1], scale[:, 1])
    m, new_state = token_mix(h, p, layer, conv_buf, ssd_conv_buf, ssd_h0, attend)
    x = x + gate[:, 1, None, :] * rms_norm(m, p['norm_post'][1])
    h = modulate(x, p['norm_pre'][2], shift[:, 2], scale[:, 2])
    f = swiglu(h, p['w_ffn_in'][1], p['w_ffn_out'][1])
    x = x + 0.5 * gate[:, 2, None, :] * rms_norm(f, p['norm_post'][2])
    return x, new_state


def setup_inputs(seed: int = 0) -> dict:
    key = jax.random.key(seed)
    ks = iter(list(jax.random.split(key, 40)))

    def nrm(shape, scale=1.0):
        return jax.random.normal(next(ks), shape, dtype=jnp.float32) * scale

    dt0 = jnp.exp(jax.random.uniform(next(ks), (DEPTH, SSD_HEADS), minval=math.log(1e-3), maxval=math.log(1e-1)))
    dt_bias = dt0 + jnp.log(-jnp.expm1(-dt0))
    a_log = jnp.log(jax.random.uniform(next(ks), (DEPTH, SSD_HEADS), minval=1.0, maxval=16.0))
    return {
        'x_prompt': nrm((BATCH, SEQ, D_MODEL)),
        'x_sample': nrm((DEC_BATCH, DEC_SEQ, D_MODEL)),
        'cache_attn_k': nrm((DEPTH, DEC_BATCH, PAST_LEN, ATT_HEADS, 2 * ATT_QK_DIM)),
        'cache_attn_v': nrm((DEPTH, DEC_BATCH, PAST_LEN, ATT_HEADS, ATT_V_DIM)),
        'state_conv': nrm((DEPTH, DEC_BATCH, CONV_WIDTH - 1, CONV_CH), 0.5),
        'state_ssd_conv': nrm((DEPTH, DEC_BATCH, SSD_CONV_WIDTH - 1, SSD_CONV_CH)),
        'state_ssd': nrm((DEPTH, DEC_BATCH, SSD_HEADS, SSD_HEAD_DIM, SSD_STATE), 0.1),
        'c_prompt': nrm((BATCH, D_MODEL)),
        'c_sample': nrm((DEC_BATCH, D_MODEL)),
        'w_ada': nrm((DEPTH, D_MODEL, N_SUB * 3 * D_MODEL), D_MODEL ** -0.5),
        'b_ada': nrm((DEPTH, N_SUB * 3 * D_MODEL), 0.01),
        'norm_pre': 1.0 + nrm((DEPTH, N_SUB, D_MODEL), 0.05),
        'norm_post': 1.0 + nrm((DEPTH, N_SUB, D_MODEL), 0.05),
        'w_ffn_in': nrm((DEPTH, 2, D_MODEL, 2 * D_FF), D_MODEL ** -0.5),
        'w_ffn_out': nrm((DEPTH, 2, D_FF, D_MODEL), D_FF ** -0.5),
        'w_in': nrm((DEPTH, D_MODEL, N_IN), D_MODEL ** -0.5),
        'b_gate': nrm((DEPTH, N_BRANCH, D_MODEL), 0.01),
        'conv_dw_w': nrm((DEPTH, CONV_WIDTH, CONV_CH), CONV_WIDTH ** -0.5),
        'conv_dw_b': nrm((DEPTH, CONV_CH), 0.01),
        'conv_ln_g': 1.0 + nrm((DEPTH, CONV_CH), 0.05),
        'conv_ln_b': nrm((DEPTH, CONV_CH), 0.01),
        'w_br_conv': nrm((DEPTH, CONV_CH, D_MODEL), CONV_CH ** -0.5),
        'ssd_conv_w': nrm((DEPTH, SSD_CONV_WIDTH, SSD_CONV_CH), SSD_CONV_WIDTH ** -0.5),
        'ssd_conv_b': nrm((DEPTH, SSD_CONV_CH), 0.01),
        'ssd_dt_bias': dt_bias,
        'ssd_A_log': a_log,
        'ssd_D': 1.0 + nrm((DEPTH, SSD_HEADS), 0.1),
        'ssd_norm_g': 1.0 + nrm((DEPTH, SSD_INNER), 0.05),
        'w_br_ssd': nrm((DEPTH, SSD_INNER, D_MODEL), SSD_INNER ** -0.5),
        'lambda_q': nrm((DEPTH, 2, ATT_QK_DIM), 0.1),
        'lambda_k': nrm((DEPTH, 2, ATT_QK_DIM), 0.1),
        'attn_subln_g': 1.0 + nrm((DEPTH, ATT_V_DIM), 0.05),
        'w_br_attn': nrm((DEPTH, ATT_WIDTH, D_MODEL), ATT_WIDTH ** -0.5),
        'w_mix_out': nrm((DEPTH, D_MODEL, D_MODEL), D_MODEL ** -0.5),
    }


def reference(x_prompt, x_sample, cache_attn_k, cache_attn_v, state_conv, state_ssd_conv, state_ssd,
              c_prompt, c_sample, w_ada, b_ada, norm_pre, norm_post, w_ffn_in, w_ffn_out, w_in, b_gate,
              conv_dw_w, conv_dw_b, conv_ln_g, conv_ln_b, w_br_conv, ssd_conv_w, ssd_conv_b, ssd_dt_bias,
              ssd_A_log, ssd_D, ssd_norm_g, w_br_ssd, lambda_q, lambda_k, attn_subln_g, w_br_attn, w_mix_out):
    bp = x_prompt.shape[0]
    dtp = x_prompt.dtype
    conv0 = jnp.zeros((bp, CONV_WIDTH - 1, CONV_CH), dtp)
    ssd_conv0 = jnp.zeros((bp, SSD_CONV_WIDTH - 1, SSD_CONV_CH), dtp)
    ssd0 = jnp.zeros((bp, SSD_HEADS, SSD_HEAD_DIM, SSD_STATE), dtp)
    xp, xs = x_prompt, x_sample
    st_p, st_s = [], []
    for l in range(DEPTH):
        p = dict(w_ada=w_ada[l], b_ada=b_ada[l], norm_pre=norm_pre[l], norm_post=norm_post[l],
                 w_ffn_in=w_ffn_in[l], w_ffn_out=w_ffn_out[l], w_in=w_in[l], b_gate=b_gate[l],
                 conv_dw_w=conv_dw_w[l], conv_dw_b=conv_dw_b[l], conv_ln_g=conv_ln_g[l], conv_ln_b=conv_ln_b[l],
                 w_br_conv=w_br_conv[l], ssd_conv_w=ssd_conv_w[l], ssd_conv_b=ssd_conv_b[l],
                 ssd_dt_bias=ssd_dt_bias[l], ssd_A_log=ssd_A_log[l], ssd_D=ssd_D[l], ssd_norm_g=ssd_norm_g[l],
                 w_br_ssd=w_br_ssd[l], lambda_q=lambda_q[l], lambda_k=lambda_k[l],
                 attn_subln_g=attn_subln_g[l], w_br_attn=w_br_attn[l], w_mix_out=w_mix_out[l])
        xp, sp = trunk_layer(xp, c_prompt, p, l, conv0, ssd_conv0, ssd0, prompt_attend)
        xs, ss = trunk_layer(xs, c_sample, p, l, state_conv[l], state_ssd_conv[l], state_ssd[l],
                             make_sample_attend(cache_attn_k[l], cache_attn_v[l]))
        st_p.append(sp)
        st_s.append(ss)
    k_prompt = jnp.stack([s[0] for s in st_p])
    v_prompt = jnp.stack([s[1] for s in st_p])
    conv_prompt = jnp.stack([s[2] for s in st_p])
    ssd_conv_prompt = jnp.stack([s[3] for s in st_p])
    ssd_prompt = jnp.stack([s[4] for s in st_p])
    k_sample = jnp.stack([s[0] for s in st_s])
    v_sample = jnp.stack([s[1] for s in st_s])
    conv_sample = jnp.stack([s[2] for s in st_s])
    ssd_conv_sample = jnp.stack([s[3] for s in st_s])
    ssd_sample = jnp.stack([s[4] for s in st_s])
    return (xp, xs, k_prompt, v_prompt, conv_prompt, ssd_conv_prompt, ssd_prompt,
            k_sample, v_sample, conv_sample, ssd_conv_sample, ssd_sample)
```

```python
import numpy as np
import concourse.bass as bass
import concourse.mybir as mybir

F32 = mybir.dt.float32
BF16 = mybir.dt.bfloat16
ALU = mybir.AluOpType
AF = mybir.ActivationFunctionType
ENGS = ("sp", "act", "dve", "pool", "pe")
DSZ = {F32: 4, BF16: 2}


class View:
    __slots__ = ("b", "ap")

    def __init__(self, b, ap):
        self.b = b
        self.ap = ap


class Buf:
    def __init__(self, name, t, shape, dtype, space, sem=None):
        self.name = name
        self.t = t
        self.shape = list(shape)
        self.dtype = dtype
        self.space = space
        self.F = int(np.prod(shape[1:])) if len(shape) > 1 else 1
        self.w = None
        self.r = {}
        self.sem = sem

    def __getitem__(self, key):
        if self.space == "dram":
            return View(self, self.t.ap()[key])
        return View(self, self.t[key])

    def raw(self, off, dims, p0=0, np_=None):
        if np_ is None:
            np_ = self.shape[0]
        return View(self, bass.AP(self.t, p0 * self.F + off, [[self.F, np_]] + [list(d) for d in dims]))

    def dap(self, ap):
        return View(self, ap)


class KB:
    def __init__(self, nc, stack):
        self.nc = nc
        self.stack = stack
        self.ops = {e: [] for e in ENGS}
        self.cnt = {}
        self.semh = {}
        self.waited = {e: {} for e in ENGS}
        self.esem = {}
        for e in ENGS:
            if e != "sp":
                self.esem[e] = self.newsem("e_" + e)
        self.dma_sems = set()
        self.shared_ld = self.newsem("d_ld", dma=True)
        self.st_sems = [self.newsem("d_st%d" % i, dma=True) for i in range(4)]
        self.st_i = 0
        self.nbuf = 0
        self.ninst = 0

    def newsem(self, name, dma=False):
        h = self.stack.enter_context(self.nc.semaphore(name))
        self.semh[name] = h
        self.cnt[name] = 0
        if dma:
            self.dma_sems.add(name)
        return name

    def sbuf(self, name, shape, dtype, own_sem=False):
        t = self.stack.enter_context(self.nc.sbuf_tensor(name, list(shape), dtype))
        sem = self.newsem("d_" + name, dma=True) if own_sem else None
        return Buf(name, t, shape, dtype, "sbuf", sem)

    def psum(self, name, shape, dtype=F32):
        t = self.stack.enter_context(self.nc.psum_tensor(name, list(shape), dtype))
        return Buf(name, t, shape, dtype, "psum")

    def dram(self, name, shape, dtype, kind="Internal", own_sem=False):
        t = self.nc.dram_tensor(name, list(shape), dtype, kind=kind)
        sem = self.newsem("d_" + name, dma=True) if own_sem else None
        return Buf(name, t, shape, dtype, "dram", sem)

    def _waits(self, eng, reads, writes):
        evs = []
        own = self.esem.get(eng)
        for b in reads:
            if b.w is not None:
                evs.append((b.w, True))
            if b.space == "psum":
                for s, v in b.r.items():
                    if s != own:
                        evs.append(((s, v), False))
        for b in writes:
            if b.w is not None:
                evs.append((b.w, False))
            for s, v in b.r.items():
                evs.append(((s, v), False))
        need = {}
        for (s, v), raw in evs:
            if s == own:
                if eng == "pe":
                    continue
            if s in self.dma_sems:
                v = self.cnt[s]
            if self.waited[eng].get(s, 0) >= v:
                continue
            if need.get(s, 0) < v:
                need[s] = v
        for s, v in need.items():
            self.waited[eng][s] = v
        return list(need.items())

    def _record(self, ev, reads, writes):
        s, v = ev
        for b in reads:
            if b.r.get(s, 0) < v:
                b.r[s] = v
        for b in writes:
            b.w = ev
            b.r = {}

    def op(self, eng, fn, reads, writes, inc=True):
        waits = self._waits(eng, reads, writes)
        s = self.esem[eng]
        ev = (s, self.cnt[s] + 1)
        if inc:
            self.cnt[s] += 1
        self.ops[eng].append((waits, fn, (s, 1) if inc else None))
        self._record(ev, reads, writes)
        self.ninst += 1

    def dma(self, q, out, in_, sem=None):
        reads, writes = [in_.b], [out.b]
        if sem is None:
            if q == "pool":
                sem = self.st_sems[self.st_i % len(self.st_sems)]
                self.st_i += 1
            else:
                sem = out.b.sem if out.b.space != "dram" and out.b.sem else None
                if sem is None:
                    sem = self.shared_ld
        waits = self._waits(q, reads, writes)
        if self.cnt[sem] > 0 and self.waited[q].get(sem, 0) < self.cnt[sem]:
            waits = [w for w in waits if w[0] != sem] + [(sem, self.cnt[sem])]
            self.waited[q][sem] = self.cnt[sem]
        self.cnt[sem] += 16
        ev = (sem, self.cnt[sem])
        oap, iap = out.ap, in_.ap
        self.ops[q].append((waits, lambda e: e.dma_start(out=oap, in_=iap), (sem, 16)))
        self._record(ev, reads, writes)
        self.ninst += 1

    def barrier(self):
        for e in ENGS:
            need = []
            for s, v in self.cnt.items():
                if v > 0 and self.waited[e].get(s, 0) < v and s != self.esem.get(e):
                    need.append((s, v))
                    self.waited[e][s] = v
            if need:
                self.ops[e].append((need, None, None))

    def mm(self, out, lhsT, rhs, start, stop, last=None):
        if last is None:
            last = stop
        o, l, r = out.ap, lhsT.ap, rhs.ap
        self.op("pe", lambda e: e.matmul(o, l, r, start=start, stop=stop), [lhsT.b, rhs.b], [out.b], inc=last)

    def tr(self, out, in_, ident, last=True):
        o, i, d = out.ap, in_.ap, ident.ap
        self.op("pe", lambda e: e.transpose(o, i, d), [in_.b, ident.b], [out.b], inc=last)

    def act(self, out, in_, func, bias=None, scale=None, accum=None):
        o, i = out.ap, in_.ap
        reads = [in_.b]
        writes = [out.b]
        kw = {}
        if bias is not None:
            if isinstance(bias, View):
                reads.append(bias.b)
                kw["bias"] = bias.ap
            else:
                kw["bias"] = float(bias)
        if scale is not None:
            if isinstance(scale, View):
                reads.append(scale.b)
                kw["scale"] = scale.ap
            else:
                kw["scale"] = float(scale)
        if accum is not None:
            writes.append(accum.b)
            kw["accum_out"] = accum.ap
        self.op("act", lambda e: e.activation(o, i, func, **kw), reads, writes)

    def copy(self, eng, out, in_):
        if eng == "act":
            return self.act(out, in_, AF.Copy)
        o, i = out.ap, in_.ap
        self.op(eng, lambda e: e.tensor_copy(o, i), [in_.b], [out.b])

    def memset(self, eng, out, val):
        o = out.ap
        self.op(eng, lambda e: e.memset(o, val), [], [out.b])

    def tt(self, eng, out, in0, in1, op):
        o, a, b = out.ap, in0.ap, in1.ap
        self.op(eng, lambda e: e.tensor_tensor(o, a, b, op), [in0.b, in1.b], [out.b])

    def ts(self, eng, out, in0, s1, s2=None, op0=ALU.mult, op1=None):
        o, a = out.ap, in0.ap
        reads = [in0.b]
        if isinstance(s1, View):
            reads.append(s1.b)
            s1 = s1.ap
        if isinstance(s2, View):
            reads.append(s2.b)
            s2 = s2.ap
        if op1 is None:
            self.op(eng, lambda e: e.tensor_scalar(o, a, s1, None, op0), reads, [out.b])
        else:
            self.op(eng, lambda e: e.tensor_scalar(o, a, s1, s2, op0, op1), reads, [out.b])

    def stt(self, eng, out, in0, scalar, in1, op0, op1):
        o, a, b = out.ap, in0.ap, in1.ap
        reads = [in0.b, in1.b]
        if isinstance(scalar, View):
            reads.append(scalar.b)
            scalar = scalar.ap
        self.op(eng, lambda e: e.scalar_tensor_tensor(o, a, scalar, b, op0, op1), reads, [out.b])

    def recip(self, out, in_):
        o, i = out.ap, in_.ap
        self.op("dve", lambda e: e.reciprocal(o, i), [in_.b], [out.b])

    def emit(self):
        nc = self.nc
        for s in list(self.dma_sems):
            v = self.cnt[s]
            if v > 0 and self.waited["sp"].get(s, 0) < v:
                self.ops["sp"].append(([(s, v)], None, None))
                self.waited["sp"][s] = v
        semh = self.semh
        ops = self.ops

        def run(e, lst):
            for waits, fn, inc in lst:
                if fn is None:
                    for s, v in waits:
                        e.wait_ge(semh[s], v)
                    continue
                for s, v in waits[:-1]:
                    e.wait_ge(semh[s], v)
                ins = fn(e)
                if waits:
                    s, v = waits[-1]
                    ins._wait_ge(semh[s], v)
                if inc is not None:
                    ins.then_inc(semh[inc[0]], inc[1])

        with nc.Block() as block:
            @block.sync
            def _(e):
                run(e, ops["sp"])

            @block.scalar
            def _(e):
                run(e, ops["act"])

            @block.vector
            def _(e):
                run(e, ops["dve"])

            @block.gpsimd
            def _(e):
                run(e, ops["pool"])

            @block.tensor
            def _(e):
                run(e, ops["pe"])


import math
from contextlib import ExitStack
from concourse.bass_utils import run_bass_kernel_spmd


D = 1024
KD = 8
FF = 2816
KF = 22
NIN = 8208
CONV_CH = 512
CW = 31
SH = 16
SP = 64
SN = 128
XBC = 1536
AH = 4
PAST = 1024
SL = 32
EPS = 1e-6
SLOPES = [2.0 ** (-8.0 * (h + 1) / AH) for h in range(AH)]
NOFF = 129
WIN = [int(math.ceil(((104.0 + 60.0) / SLOPES[h] + 64.0) / 128.0)) * 128 for h in range(AH)]


def lam0_of(l):
    return 0.8 - 0.6 * math.exp(-0.3 * l)


class Seq:
    pass


class Kern:
    def __init__(self, SEQ, NS=2, DEPTH=2, BL=256):
        self.SEQ, self.NS, self.DEPTH, self.BL = SEQ, NS, DEPTH, BL
        self.nc = bass.Bass("TRN2", target_bir_lowering=False)
        try:
            self.nc.allow_low_precision("bf16 matmul operands with fp32 accumulation (problem tolerance)")
        except Exception:
            pass

    def build(self):
        with ExitStack() as st:
            self.k = KB(self.nc, st)
            import os
            stage = int(os.environ.get("KSTAGE", "99"))
            self.declare()
            self.setup_consts()
            if stage >= 1:
                self.precast()
            for l in range(self.DEPTH):
                if stage >= 2:
                    self.layer_setup(l)
                if stage >= 3:
                    for sq in self.seqs:
                        nb = sq.T // sq.L
                        for b in range(nb):
                            self.run_block(l, sq, b)
            self.k.emit()
        return self.nc

    def declare(self):
        k, SEQ, NS, DEPTH, BL = self.k, self.SEQ, self.NS, self.DEPTH, self.BL
        I = lambda n, s: k.dram(n, s, F32, kind="ExternalInput")
        O = lambda n, s: k.dram(n, s, F32, kind="ExternalOutput")
        self.xp = I("xp", [SEQ, D]); self.cp = I("cp", [1, D])
        self.xs = I("xs", [NS * SL, D]); self.cs = I("cs", [NS, D])
        self.ck = I("ck", [DEPTH * NS * PAST, 512]); self.cv = I("cv", [DEPTH * NS * PAST, 512])
        self.sconv = I("sconv", [DEPTH * NS * 30, 512]); self.sssdc = I("sssdc", [DEPTH * NS * 3, XBC])
        self.sssd = I("sssd", [DEPTH * NS * 1024, 128])
        self.w_ada = I("w_ada", [DEPTH * D, 9216]); self.b_ada = I("b_ada", [DEPTH * 72, 128])
        self.norm_pre = I("norm_pre", [DEPTH * 24, 128]); self.norm_post = I("norm_post", [DEPTH * 24, 128])
        self.w_ffn_in = I("w_ffn_in", [DEPTH * 2 * D, 2 * FF]); self.w_ffn_out = I("w_ffn_out", [DEPTH * 2 * FF, D])
        self.w_in = I("w_in", [DEPTH * D, NIN]); self.b_gate = I("b_gate", [DEPTH * 24, 128])
        self.conv_dw_w = I("conv_dw_w", [DEPTH * 124, 128]); self.conv_dw_b = I("conv_dw_b", [DEPTH * 4, 128])
        self.conv_ln_g = I("conv_ln_g", [DEPTH * 4, 128]); self.conv_ln_b = I("conv_ln_b", [DEPTH * 4, 128])
        self.w_br_conv = I("w_br_conv", [DEPTH * 512, D])
        self.ssd_conv_w = I("ssd_conv_w", [DEPTH * 48, 128]); self.ssd_conv_b = I("ssd_conv_b", [DEPTH * 12, 128])
        self.ssd_dt_bias = I("ssd_dt_bias", [DEPTH, 16]); self.ssd_A_log = I("ssd_A_log", [DEPTH, 16])
        self.ssd_D = I("ssd_D", [DEPTH, 16]); self.ssd_norm_g = I("ssd_norm_g", [DEPTH * 8, 128])
        self.w_br_ssd = I("w_br_ssd", [DEPTH * D, D])
        self.lambda_q = I("lambda_q", [DEPTH * 2, 64]); self.lambda_k = I("lambda_k", [DEPTH * 2, 64])
        self.attn_subln_g = I("attn_subln_g", [DEPTH, 128])
        self.w_br_attn = I("w_br_attn", [DEPTH * 512, D]); self.w_mix_out = I("w_mix_out", [DEPTH * D, D])
        self.c_ident = I("c_ident", [128, 128]); self.c_lt = I("c_lt", [64, 64]); self.c_ut = I("c_ut", [64, 64])
        self.c_mk = I("c_mk", [64, 64]); self.c_alb = I("c_alb", [128, AH * NOFF]); self.c_cm = I("c_cm", [128, AH * 128])
        self.c_pm = I("c_pm", [2, 128])
        self.o_yp = O("o_yp", [SEQ, D]); self.o_ys = O("o_ys", [NS * SL, D])
        self.o_kp = O("o_kp", [DEPTH * SEQ, 512]); self.o_vp = O("o_vp", [DEPTH * SEQ, 512])
        self.o_convp = O("o_convp", [DEPTH * 30, 512]); self.o_ssdcp = O("o_ssdcp", [DEPTH * 3, XBC])
        self.o_ssdp = O("o_ssdp", [DEPTH * 1024, 128])
        self.o_ks = O("o_ks", [DEPTH * NS * SL, 512]); self.o_vs = O("o_vs", [DEPTH * NS * SL, 512])
        self.o_convs = O("o_convs", [DEPTH * NS * 30, 512]); self.o_ssdcs = O("o_ssdcs", [DEPTH * NS * 3, XBC])
        self.o_ssds = O("o_ssds", [DEPTH * NS * 1024, 128])
        self.x_scr = k.dram("x_scr", [128, KD * (SEQ + NS * SL)], F32)
        self.Kh = k.dram("Kh", [DEPTH * AH * 128, SEQ], BF16)
        self.Vh = k.dram("Vh", [DEPTH * SEQ, AH * 129], BF16)
        self.wspec = {}
        def W(name, nslab, elems):
            self.wspec[name] = (k.dram("ws_" + name, [DEPTH * nslab * 128, elems], BF16), nslab, elems)
        for f in range(2):
            W("ffn_in%d" % f, 22, KD * 256)
            W("ffn_out%d" % f, 8, KF * 128)
        W("conv", 4, KD * 256); W("z", 8, KD * 128); W("xbc", 12, KD * 128); W("dt", 1, KD * 16)
        W("qkv", 12, KD * 128); W("gate", 8, KD * 384); W("br", 8, 16 * 128); W("mix", 8, KD * 128)
        self.seqs = []
        s = Seq(); s.idx = 0; s.sample = False; s.T = SEQ; s.L = min(BL, SEQ); s.Q = 64; s.xoff = 0
        self.seqs.append(s)
        for i in range(NS):
            s = Seq(); s.idx = 1 + i; s.sample = True; s.T = SL; s.L = SL; s.Q = SL; s.xoff = SEQ + i * SL
            self.seqs.append(s)
        self.NSEQ = 1 + NS
        L = BL
        S = k.sbuf
        self.ident = S("ident", [128, 128], F32); self.ones_b = S("ones_b", [128, 128], BF16)
        self.ones_f = S("ones_f", [128, 128], F32)
        self.lt = S("lt", [64, 64], F32); self.ut = S("ut", [64, 64], F32); self.mk = S("mk", [64, 64], F32)
        self.alb = S("alb", [128, AH * NOFF], F32); self.cm = S("cm", [128, AH * 128], BF16)
        self.pm = S("pm", [2, 128], F32)
        self.xT = S("xT", [128, KD, L], F32); self.hT = S("hT", [128, KD, L], BF16)
        self.gT = S("gT", [128, KF, L], BF16); self.fT = S("fT", [128, KD, L], F32)
        self.sqT = S("sqT", [128, KD, L], BF16); self.rstd = S("rstd", [128, 2, L], F32)
        self.tmpA = [S("tmpA%d" % i, [128, L], F32) for i in range(3)]
        self.tmp3 = S("tmp3", [128, 3, L], F32)
        self.tmpm = self.tmpA
        self.xstage = [S("xstage%d" % i, [128, D], F32) for i in range(2)]
        self.pre = [S("pre%d" % i, [128, 3 + L], F32) for i in range(2)]
        self.xhalo = [S("xhalo%d" % i, [128, 12, 3], F32) for i in range(self.NSEQ)]
        self.xbcT = S("xbcT", [128, 12, L], F32, own_sem=True); self.bcb = S("bcb", [128, 4, L], BF16)
        self.aT = [S("aT%d" % i, [128, 4, 30 + L], F32) for i in range(1)]
        self.ahalo = [S("ahalo%d" % i, [128, 4, 30], F32) for i in range(self.NSEQ)]
        self.cvv = S("cvv", [128, 4, L], F32)
        self.cvT = S("cvT", [128, 4, L], BF16)
        self.yT = S("yT", [128, 12, L], F32, own_sem=True); self.ynT = S("ynT", [128, KD, L], BF16)
        self.wf32 = [self.xbcT, self.yT]
        self.qT = S("qT", [128, AH, L], BF16); self.kTb = S("kTb", [128, AH, L], BF16)
        self.kvf = [S("kvf%d" % i, [128, L], F32) for i in range(2)]
        self.k_tm = S("k_tm", [128, max(1, L // 128), 512], F32); self.v_tm = self.k_tm
        self.v_tmb = S("v_tmb", [128, max(1, L // 128), AH, 129], BF16)
        self.Kg = [S("Kg%d" % i, [128, 512], BF16, own_sem=True) for i in range(2)]
        self.Vg = [S("Vg%d" % i, [128, 4, 129], BF16, own_sem=True) for i in range(2)]
        self.PT = [S("PT%d" % i, [128, L], BF16) for i in range(3)]
        self.oT = S("oT", [128, AH, L], BF16)
        self.osm = S("osm", [128, 16], F32); self.o0 = S("o0", [128, 128], F32); self.o1 = S("o1", [128, 128], F32)
        self.osq = S("osq", [128, 128], F32)
        self.KsT = S("KsT", [128, PAST + SL], BF16); self.Vs = S("Vs", [128, 9, 129], BF16)
        self.cst = S("cst", [128, 512], F32)
        self.mergedT = S("mergedT", [128, KD, L], BF16)
        self.NSLOT = 3
        self.wslot = [S("wslot%d" % i, [128, 3072], BF16, own_sem=True) for i in range(self.NSLOT)]
        self.s_dt = S("s_dt", [64, 16], F32); self.s_dtA = S("s_dtA", [64, 16], F32)
        self.s_r1 = S("s_r1", [64, 16, 64], F32); self.s_E = self.s_r1
        self.s_sm = S("s_sm", [128, 64], F32)
        self.s_xs = S("s_xs", [64, 1024], F32); self.s_xdt = S("s_xdt", [64, 1024], BF16)
        self.s_xdte = S("s_xdte", [64, 1024], BF16); self.s_btm = S("s_btm", [64, 256], BF16)
        self.s_cbm = S("s_cbm", [64, 2, 64], F32); self.s_MT = S("s_MT", [64, 16, 64], BF16)
        self.s_y = S("s_y", [64, 1024], F32); self.s_t = S("s_t", [64, 1024], F32)
        hs = [S("hst%d" % i, [128, 1024], F32) for i in range(2)]
        self.hst = [hs[0]] + [hs[1]] * self.NS
        self.hstb = S("hstb", [128, 1024], BF16)
        self.pstage = [S("pstage%d" % i, [128, 128], F32) for i in range(3)]
        self.colp = S("colp", [128, 384], F32)
        self.rowp = S("rowp", [128, 16 * 3 + 128], F32)
        self.cstage = S("cstage", [32, 128], F32); self.scT = S("scT", [128, 24], F32)
        self.modT = S("modT", [128, 72, 3], F32)
        self.preA = S("preA", [128, 3, 3, KD], F32); self.preB = S("preB", [128, 3, 3, KD], F32)
        self.postC = S("postC", [128, 3, 3, KD], F32)
        self.lamc = S("lamc", [128, 4], F32); self.lqk = S("lqk", [2, 192], F32)
        self.ps = [k.psum("ps%d" % i, [128, 512]) for i in range(8)]
        self.psi = 0
        self.wq_i = 0
        self.wq_loaded = {}

    def bank(self):
        b = self.ps[self.psi % 4]
        self.psi += 1
        return b

    def setup_consts(self):
        k = self.k
        k.dma("sp", self.ident[:, :], self.c_ident[:, :])
        k.dma("sp", self.lt[:, :], self.c_lt[:, :]); k.dma("sp", self.ut[:, :], self.c_ut[:, :])
        k.dma("sp", self.mk[:, :], self.c_mk[:, :]); k.dma("sp", self.alb[:, :], self.c_alb[:, :])
        k.dma("sp", self.pm[:, :], self.c_pm[:, :])
        k.dma("sp", self.cst[:, :], self.c_cm[:, :])
        k.copy("dve", self.cm[:, :], self.cst[:, :])
        k.memset("dve", self.ones_b[:, :], 1.0); k.memset("dve", self.ones_f[:, :], 1.0)
        k.memset("pool", self.v_tmb[:, :, :, :], 1.0)
        k.memset("pool", self.Vs[:, :, :], 1.0)

    def precast(self):
        k = self.k
        DEPTH = self.DEPTH
        self.pc_i = 0
        def slab(name, l, j, K, segs, src, rowbase):
            buf, nslab, elems = self.wspec[name]
            tot = sum(n for _, n in segs)
            assert K * tot == elems, (name, K, tot, elems)
            stg = self.wf32_big[self.pc_i % 2]
            slot = self.wslot[self.pc_i % self.NSLOT]
            off = 0
            for (c0, n) in segs:
                sv = src.dap(src.t.ap()[rowbase:rowbase + K * 128, c0:c0 + n].rearrange("(kc p) n -> p kc n", p=128))
                dv = stg.raw(off, [(tot, K), (1, n)])
                k.dma("sp", dv, sv)
                off += n
            eng = ("dve", "act", "pool")[self.pc_i % 3]
            k.copy(eng, slot.raw(0, [(1, elems)]), stg.raw(0, [(1, elems)]))
            r0 = (l * nslab + j) * 128
            k.dma("pool", buf.dap(buf.t.ap()[r0:r0 + 128, :]), slot.raw(0, [(1, elems)]))
            self.pc_i += 1
        self.wf32_big = self.wf32
        for l in range(DEPTH):
            for f in range(2):
                rb = (l * 2 + f) * D
                for j in range(22):
                    slab("ffn_in%d" % f, l, j, KD, [(128 * j, 128), (FF + 128 * j, 128)], self.w_ffn_in, rb)
                rb = (l * 2 + f) * FF
                for j in range(8):
                    slab("ffn_out%d" % f, l, j, KF, [(128 * j, 128)], self.w_ffn_out, rb)
            rb = l * D
            for j in range(4):
                slab("conv", l, j, KD, [(128 * j, 128), (512 + 128 * j, 128)], self.w_in, rb)
            for j in range(8):
                slab("z", l, j, KD, [(1024 + 128 * j, 128)], self.w_in, rb)
            for j in range(12):
                slab("xbc", l, j, KD, [(2048 + 128 * j, 128)], self.w_in, rb)
            slab("dt", l, 0, KD, [(3584, 16)], self.w_in, rb)
            for j in range(12):
                slab("qkv", l, j, KD, [(3600 + 128 * j, 128)], self.w_in, rb)
            for j in range(8):
                slab("gate", l, j, KD, [(5136 + i * 1024 + 128 * j, 128) for i in range(3)], self.w_in, rb)
            for j in range(8):
                buf, nslab, elems = self.wspec["br"]
                stg = self.wf32_big[self.pc_i % 2]
                slot = self.wslot[self.pc_i % self.NSLOT]
                pieces = [(self.w_br_conv, l * 512, 4, 0), (self.w_br_ssd, l * D, 8, 4), (self.w_br_attn, l * 512, 4, 12)]
                for (src, rbb, K, kc0) in pieces:
                    sv = src.dap(src.t.ap()[rbb:rbb + K * 128, 128 * j:128 * j + 128].rearrange("(kc p) n -> p kc n", p=128))
                    k.dma("sp", stg.raw(kc0 * 128, [(128, K), (1, 128)]), sv)
                eng = ("dve", "act", "pool")[self.pc_i % 3]
                k.copy(eng, slot.raw(0, [(1, elems)]), stg.raw(0, [(1, elems)]))
                r0 = (l * nslab + j) * 128
                k.dma("pool", buf.dap(buf.t.ap()[r0:r0 + 128, :]), slot.raw(0, [(1, elems)]))
                self.pc_i += 1
            for j in range(8):
                slab("mix", l, j, KD, [(128 * j, 128)], self.w_mix_out, l * D)

    def wslab(self, name, l, j):
        k = self.k
        buf, nslab, elems = self.wspec[name]
        slot = self.wslot[self.wq_i % self.NSLOT]
        self.wq_i += 1
        r0 = (l * nslab + j) * 128
        k.dma("sp", slot.raw(0, [(1, elems)]), buf.dap(buf.t.ap()[r0:r0 + 128, :]))
        return slot

    def layer_setup(self, l):
        k = self.k
        NSEQ = self.NSEQ
        rows = [(self.norm_pre, 24), (self.norm_post, 24), (self.b_ada, 72), (self.b_gate, 24), (self.conv_dw_b, 4),
                (self.conv_ln_g, 4), (self.conv_ln_b, 4), (self.ssd_conv_b, 12), (self.ssd_norm_g, 8),
                (self.ssd_conv_w, 48), (self.conv_dw_w, 124)]
        self.CP = {}
        pos = 0
        flat = []
        for (src, n) in rows:
            self.CP[src.name] = pos
            for r in range(n):
                flat.append((src, l * n + r))
            pos += n
        assert pos == 348
        i = 0
        while i < len(flat):
            src, r0 = flat[i]
            tile_i, prow = divmod(i, 128)
            n = 1
            while i + n < len(flat) and flat[i + n][0] is src and (i + n) // 128 == tile_i:
                n += 1
            k.dma("sp", self.pstage[tile_i][prow:prow + n, :], src[r0:r0 + n, :])
            i += n
        for t in range(3):
            nr = min(128, 348 - t * 128)
            pb = self.ps[4 + t]
            k.tr(pb[:, 0:nr], self.pstage[t][0:nr, :], self.ident[0:nr, 0:nr])
            k.copy("dve", self.colp[:, t * 128:t * 128 + nr], pb[:, 0:nr])
        def bc(src, row, n):
            return src.dap(bass.AP(src.t, row * n, [[0, 128], [1, n]]))
        k.dma("sp", self.rowp[:, 0:16], bc(self.ssd_dt_bias, l, 16))
        k.dma("sp", self.rowp[:, 16:32], bc(self.ssd_A_log, l, 16))
        k.dma("sp", self.rowp[:, 32:48], bc(self.ssd_D, l, 16))
        k.dma("sp", self.rowp[:, 48:176], bc(self.attn_subln_g, l, 128))
        k.act(self.rowp[:, 16:32], self.rowp[:, 16:32], AF.Exp)
        k.ts("dve", self.rowp[:, 16:32], self.rowp[:, 16:32], -1.0, None, ALU.mult)
        k.ts("dve", self.rowp[:, 48:176], self.rowp[:, 48:176], 1.0 - lam0_of(l), None, ALU.mult)
        k.dma("sp", self.lqk[:, 0:64], self.lambda_q[2 * l:2 * l + 2, :])
        k.dma("sp", self.lqk[:, 64:128], self.lambda_k[2 * l:2 * l + 2, :])
        k.tt("dve", self.lqk[:, 128:192], self.lqk[:, 0:64], self.lqk[:, 64:128], ALU.mult)
        k.act(self.lqk[:, 0:64], self.lqk[:, 128:192], AF.Copy, accum=self.lqk[:, 64:65])
        k.act(self.lqk[:, 65:66], self.lqk[:, 64:65], AF.Exp)
        pb = self.ps[7]
        k.mm(pb[:, 0:1], self.pm[:, :], self.lqk[:, 65:66], True, True)
        k.ts("dve", self.lamc[:, 0:1], pb[:, 0:1], -1.0, -lam0_of(l), ALU.mult, ALU.add)
        k.dma("sp", self.cstage[0:8, :], self.cp.dap(self.cp.t.ap().rearrange("o (k p) -> (o k) p", p=128)))
        k.dma("sp", self.cstage[8:8 + 8 * self.NS, :], self.cs.dap(self.cs.t.ap().rearrange("s (k p) -> (s k) p", p=128)))
        nr = 8 * NSEQ
        pb = self.ps[4]
        k.tr(pb[:, 0:nr], self.cstage[0:nr, :], self.ident[0:nr, 0:nr])
        k.act(self.scT[:, 0:nr], pb[:, 0:nr], AF.Silu)
        pm = self.ps[5]
        for j4 in range(18):
            for hf in range(2):
                wb = self.wf32[(2 * j4 + hf) % 2]
                sv = self.w_ada.dap(self.w_ada.t.ap()[l * D:(l + 1) * D, 512 * j4 + 256 * hf:512 * j4 + 256 * hf + 256].rearrange("(kc p) n -> p kc n", p=128))
                k.dma("sp", wb.raw(0, [(256, 8), (1, 256)]), sv)
                for jj in range(2):
                    j = 4 * j4 + 2 * hf + jj
                    for kc in range(KD):
                        k.mm(pm[:, 3 * j:3 * j + NSEQ], wb.raw(kc * 256 + jj * 128, [(1, 128)]),
                             self.scT.raw(kc, [(8, NSEQ)]), kc == 0, kc == KD - 1)
        cb = self.CP["b_ada"]
        k.tt("dve", self.modT[:, :, 0:NSEQ], pm.raw(0, [(3, 72), (1, NSEQ)]),
             self.colp.raw(cb, [(1, 72), (0, NSEQ)]), ALU.add)
        cpre, cpost = self.CP["norm_pre"], self.CP["norm_post"]
        for sub in range(3):
            for sq in range(NSEQ):
                sh = self.modT.raw(((sub * 3 + 0) * 8) * 3 + sq, [(3, KD)])
                sc = self.modT.raw(((sub * 3 + 1) * 8) * 3 + sq, [(3, KD)])
                gt = self.modT.raw(((sub * 3 + 2) * 8) * 3 + sq, [(3, KD)])
                pa = self.preA[:, sub, sq, :]; pbb = self.preB[:, sub, sq, :]; pc = self.postC[:, sub, sq, :]
                k.stt("dve", pa, sc, 1.0, self.colp[:, cpre + sub * 8:cpre + sub * 8 + 8], ALU.add, ALU.mult)
                k.copy("dve", pbb, sh)
                k.stt("dve", pc, gt, 1.0 if sub == 1 else 0.5, self.colp[:, cpost + sub * 8:cpost + sub * 8 + 8], ALU.mult, ALU.mult)

    def rms_stats(self, src, nk0, nk1, L, which, eps, div):
        k = self.k
        k.act(self.sqT[:, nk0:nk1, 0:L], src[:, nk0:nk1, 0:L], AF.Square)
        pb = self.ps[6 + (which % 2)]
        for kk in range(nk0, nk1):
            k.mm(pb[:, 0:L], self.ones_b[:, :], self.sqT[:, kk, 0:L], kk == nk0, kk == nk1 - 1)
        k.act(self.rstd[:, which, 0:L], pb[:, 0:L], AF.Sqrt, bias=eps, scale=1.0 / div)
        k.recip(self.rstd[:, which, 0:L], self.rstd[:, which, 0:L])

    def prenorm(self, sub, sq, L):
        k = self.k
        self.rms_stats(self.xT, 0, KD, L, 0, EPS, float(D))
        for kk in range(KD):
            t = self.tmpA[kk % 3]
            k.stt("dve", t[:, 0:L], self.xT[:, kk, 0:L], self.preA[:, sub, sq.idx, kk:kk + 1], self.rstd[:, 0, 0:L], ALU.mult, ALU.mult)
            k.ts("pool", self.hT[:, kk, 0:L], t[:, 0:L], self.preB[:, sub, sq.idx, kk:kk + 1], None, ALU.add)

    def postnorm(self, sub, sq, L):
        k = self.k
        self.rms_stats(self.fT, 0, KD, L, 1, EPS, float(D))
        for kk in range(KD):
            t = self.tmpA[kk % 3]
            k.stt("dve", t[:, 0:L], self.fT[:, kk, 0:L], self.postC[:, sub, sq.idx, kk:kk + 1], self.rstd[:, 1, 0:L], ALU.mult, ALU.mult)
            k.tt("pool", self.xT[:, kk, 0:L], self.xT[:, kk, 0:L], t[:, 0:L], ALU.add)

    def proj(self, slot, off, K, rhs_buf, L, step=None, ncols=128):
        k = self.k
        pb = self.bank()
        for kk in range(K):
            k.mm(pb[0:ncols, 0:L], slot.raw(kk * step + off, [(1, ncols)]), rhs_buf[:, kk, 0:L], kk == 0, kk == K - 1)
        return pb

    def load_x(self, l, sq, b):
        k = self.k
        L = sq.L
        t0 = b * L
        if l == 0:
            src = self.xs if sq.sample else self.xp
            base = (sq.idx - 1) * SL if sq.sample else 0
            for s in range((L + 127) // 128):
                n = min(128, L - s * 128)
                stg = self.xstage[s % 2]
                k.dma("sp", stg[0:n, :], src[base + t0 + s * 128:base + t0 + s * 128 + n, :])
                for kk in range(KD):
                    pb = self.bank()
                    k.tr(pb[:, 0:n], stg[0:n, kk * 128:(kk + 1) * 128], self.ident[0:n, 0:n])
                    k.copy("act" if kk % 2 else "dve", self.xT[:, kk, s * 128:s * 128 + n], pb[:, 0:n])
        else:
            TT = self.SEQ + self.NS * SL
            sv = self.x_scr.dap(bass.AP(self.x_scr.t, sq.xoff + t0, [[KD * TT, 128], [TT, KD], [1, L]]))
            k.dma("sp", self.xT[:, :, 0:L], sv)

    def store_x(self, l, sq, b):
        k = self.k
        L = sq.L
        t0 = b * L
        if l < self.DEPTH - 1:
            TT = self.SEQ + self.NS * SL
            dv = self.x_scr.dap(bass.AP(self.x_scr.t, sq.xoff + t0, [[KD * TT, 128], [TT, KD], [1, L]]))
            k.dma("pool", dv, self.xT[:, :, 0:L])
        else:
            dst = self.o_ys if sq.sample else self.o_yp
            base = (sq.idx - 1) * SL if sq.sample else 0
            for s in range((L + 127) // 128):
                n = min(128, L - s * 128)
                stg = self.xstage[s % 2]
                for kk in range(KD):
                    pb = self.bank()
                    k.tr(pb[0:n, 0:128], self.xT[:, kk, s * 128:s * 128 + n], self.ident[:, :])
                    k.copy("act" if kk % 2 else "dve", stg[0:n, kk * 128:(kk + 1) * 128], pb[0:n, 0:128])
                k.dma("pool", dst[base + t0 + s * 128:base + t0 + s * 128 + n, :], stg[0:n, :])

    def ffn(self, l, f, sq, L):
        k = self.k
        sub = 0 if f == 0 else 2
        self.prenorm(sub, sq, L)
        for j in range(KF):
            slot = self.wslab("ffn_in%d" % f, l, j)
            pu = self.proj(slot, 0, KD, self.hT, L, step=256)
            pg = self.proj(slot, 128, KD, self.hT, L, step=256)
            t = self.tmpm[j % 3]
            k.act(t[:, 0:L], pg[:, 0:L], AF.Silu)
            k.tt("dve", self.gT[:, j, 0:L], t[:, 0:L], pu[:, 0:L], ALU.mult)
        for m in range(KD):
            slot = self.wslab("ffn_out%d" % f, l, m)
            pb = self.proj(slot, 0, KF, self.gT, L, step=128)
            k.copy("act" if m % 2 else "dve", self.fT[:, m, 0:L], pb[:, 0:L])
        self.postnorm(sub, sq, L)

    def run_block(self, l, sq, b):
        import os
        stage = int(os.environ.get("KSTAGE", "99"))
        self.stage = stage
        L = sq.L
        self.load_x(l, sq, b)
        if stage >= 4:
            self.ffn(l, 0, sq, L)
        if stage >= 5:
            self.mixer(l, sq, b)
        if stage >= 10:
            self.ffn(l, 1, sq, L)
        self.store_x(l, sq, b)

    def mixer(self, l, sq, b):
        k = self.k
        L = sq.L
        self.prenorm(1, sq, L)
        if self.stage >= 6:
            self.conv_branch(l, sq, b)
        if self.stage >= 7:
            self.ssd_branch(l, sq, b)
        if self.stage >= 8:
            self.attn_branch(l, sq, b)
        if self.stage >= 9:
            self.merge(l, sq, L)
            self.postnorm(1, sq, L)

    def conv_branch(self, l, sq, b):
        k = self.k
        L = sq.L
        nb = sq.T // L
        aT = self.aT[0]
        ah = self.ahalo[sq.idx]
        if b == 0:
            if sq.sample:
                r0 = (l * self.NS + (sq.idx - 1)) * 30
                k.dma("sp", self.cst[0:30, :], self.sconv[r0:r0 + 30, :])
                for j in range(4):
                    pb = self.bank()
                    k.tr(pb[:, 0:30], self.cst[0:30, j * 128:(j + 1) * 128], self.ident[0:30, 0:30])
                    k.copy("dve", aT[:, j, 0:30], pb[:, 0:30])
            else:
                k.memset("pool", aT[:, :, 0:30], 0.0)
        else:
            k.copy("pool", aT[:, :, 0:30], ah[:, :, :])
        for j in range(4):
            slot = self.wslab("conv", l, j)
            pa = self.proj(slot, 0, KD, self.hT, L, step=256)
            pg = self.proj(slot, 128, KD, self.hT, L, step=256)
            t = self.tmpm[j % 3]
            k.act(t[:, 0:L], pg[:, 0:L], AF.Sigmoid)
            k.tt("dve", aT[:, j, 30:30 + L], t[:, 0:L], pa[:, 0:L], ALU.mult)
        k.copy("pool", ah[:, :, :], aT[:, :, L:L + 30])
        cw, cb = self.CP["conv_dw_w"], self.CP["conv_dw_b"]
        for j in range(4):
            k.ts("dve", self.cvv[:, j, 0:L], aT[:, j, 0:L], self.colp[:, cw + j:cw + j + 1], self.colp[:, cb + j:cb + j + 1], ALU.mult, ALU.add)
            for tap in range(1, CW):
                c = cw + tap * 4 + j
                k.stt("dve", self.cvv[:, j, 0:L], aT[:, j, tap:tap + L], self.colp[:, c:c + 1], self.cvv[:, j, 0:L], ALU.mult, ALU.add)
        if b == nb - 1:
            for j in range(4):
                pb = self.bank()
                k.tr(pb[0:30, 0:128], aT[:, j, L:L + 30], self.ident[:, :])
                k.copy("dve", self.cst[0:30, j * 128:(j + 1) * 128], pb[0:30, 0:128])
            if sq.sample:
                r0 = (l * self.NS + (sq.idx - 1)) * 30
                k.dma("pool", self.o_convs[r0:r0 + 30, :], self.cst[0:30, :])
            else:
                k.dma("pool", self.o_convp[l * 30:l * 30 + 30, :], self.cst[0:30, :])
        k.act(self.yT[:, 8:12, 0:L], self.cvv[:, :, 0:L], AF.Square)
        p1 = self.ps[6]; p2 = self.ps[7]
        for j in range(4):
            k.mm(p1[:, 0:L], self.ones_f[:, :], self.cvv[:, j, 0:L], j == 0, j == 3)
        for j in range(4):
            k.mm(p2[:, 0:L], self.ones_f[:, :], self.yT[:, 8 + j, 0:L], j == 0, j == 3)
        mean = self.tmp3[:, 0, 0:L]; var = self.tmp3[:, 1, 0:L]; rs = self.tmp3[:, 2, 0:L]
        k.ts("dve", mean, p1[:, 0:L], 1.0 / 512, None, ALU.mult)
        k.tt("dve", var, mean, mean, ALU.mult)
        k.stt("dve", var, p2[:, 0:L], 1.0 / 512, var, ALU.mult, ALU.subtract)
        k.act(rs, var, AF.Sqrt, bias=1e-5, scale=1.0)
        k.recip(rs, rs)
        lg, lb = self.CP["conv_ln_g"], self.CP["conv_ln_b"]
        for j in range(4):
            t = self.tmpA[j % 3]
            k.tt("dve", t[:, 0:L], self.cvv[:, j, 0:L], mean, ALU.subtract)
            k.tt("pool", t[:, 0:L], t[:, 0:L], rs, ALU.mult)
            k.ts("dve", t[:, 0:L], t[:, 0:L], self.colp[:, lg + j:lg + j + 1], self.colp[:, lb + j:lb + j + 1], ALU.mult, ALU.add)
            k.act(self.cvT[:, j, 0:L], t[:, 0:L], AF.Silu)

    def sub(self, n):
        import os
        return float(os.environ.get("KSUB", "99")) < n

    def ssd_branch(self, l, sq, b):
        k = self.k
        L, Q = sq.L, sq.Q
        nb = sq.T // L
        xh = self.xhalo[sq.idx]
        hst = self.hst[sq.idx]
        if b == 0:
            if sq.sample:
                r0 = (l * self.NS + (sq.idx - 1)) * 3
                for c3 in range(3):
                    k.dma("sp", self.cst[0:3, :], self.sssdc[r0:r0 + 3, c3 * 512:(c3 + 1) * 512])
                    for jj in range(4):
                        j = c3 * 4 + jj
                        pb = self.bank()
                        k.tr(pb[:, 0:3], self.cst[0:3, jj * 128:(jj + 1) * 128], self.ident[0:3, 0:3])
                        k.copy("dve", xh[:, j, :], pb[:, 0:3])
                r0 = (l * self.NS + (sq.idx - 1)) * 1024
                for jj in range(8):
                    stg = self.xstage[jj % 2]
                    k.dma("sp", stg[:, 0:128], self.sssd[r0 + jj * 128:r0 + (jj + 1) * 128, :])
                    pb = self.bank()
                    k.tr(pb[:, 0:128], stg[:, 0:128], self.ident[:, :])
                    k.copy("dve", hst[:, jj * 128:(jj + 1) * 128], pb[:, 0:128])
            else:
                k.memset("pool", xh[:, :, :], 0.0)
                k.memset("pool", hst[:, :], 0.0)
            k.copy("act", self.hstb[:, :], hst[:, :])
        if self.sub(1): return
        cw, cb = self.CP["ssd_conv_w"], self.CP["ssd_conv_b"]
        for j in range(12):
            slot = self.wslab("xbc", l, j)
            pb = self.proj(slot, 0, KD, self.hT, L, step=128)
            pre = self.pre[j % 2]
            k.copy("act", pre[:, 3:3 + L], pb[:, 0:L])
            k.copy("pool", pre[:, 0:3], xh[:, j, :])
            k.copy("pool", xh[:, j, :], pre[:, L:L + 3])
            t = self.tmpA[j % 3]
            k.ts("dve", t[:, 0:L], pre[:, 0:L], self.colp[:, cw + j:cw + j + 1], self.colp[:, cb + j:cb + j + 1], ALU.mult, ALU.add)
            for tap in range(1, 4):
                c = cw + tap * 12 + j
                k.stt("dve", t[:, 0:L], pre[:, tap:tap + L], self.colp[:, c:c + 1], t[:, 0:L], ALU.mult, ALU.add)
            k.act(self.xbcT[:, j, 0:L], t[:, 0:L], AF.Silu)
        if self.sub(2): return
        if b == nb - 1:
            for c3 in range(3):
                for jj in range(4):
                    j = c3 * 4 + jj
                    pb = self.bank()
                    k.tr(pb[0:3, 0:128], xh[:, j, :], self.ident[:, :])
                    k.copy("dve", self.cst[0:3, jj * 128:(jj + 1) * 128], pb[0:3, 0:128])
                if sq.sample:
                    r0 = (l * self.NS + (sq.idx - 1)) * 3
                    k.dma("pool", self.o_ssdcs[r0:r0 + 3, c3 * 512:(c3 + 1) * 512], self.cst[0:3, :])
                else:
                    k.dma("pool", self.o_ssdcp[l * 3:l * 3 + 3, c3 * 512:(c3 + 1) * 512], self.cst[0:3, :])
        if self.sub(3): return
        k.copy("pool", self.bcb[:, :, 0:L], self.xbcT[:, 8:12, 0:L])
        dslot = self.wslab("dt", l, 0)
        for c in range(L // Q):
            self.ssd_chunk(l, sq, c, dslot, hst)
        if self.sub(13): return
        if b == nb - 1:
            for jj in range(8):
                pb = self.bank()
                k.tr(pb[:, 0:128], hst[:, jj * 128:(jj + 1) * 128], self.ident[:, :])
                stg = self.xstage[jj % 2]
                k.copy("dve", stg[:, 0:128], pb[:, 0:128])
                if sq.sample:
                    r0 = (l * self.NS + (sq.idx - 1)) * 1024
                    k.dma("pool", self.o_ssds[r0 + jj * 128:r0 + (jj + 1) * 128, :], stg[:, 0:128])
                else:
                    k.dma("pool", self.o_ssdp[l * 1024 + jj * 128:l * 1024 + (jj + 1) * 128, :], stg[:, 0:128])
        if self.sub(14): return
        for j in range(8):
            slot = self.wslab("z", l, j)
            pb = self.proj(slot, 0, KD, self.hT, L, step=128)
            t = self.tmpm[j % 3]
            k.act(t[:, 0:L], pb[:, 0:L], AF.Silu)
            k.tt("dve", self.yT[:, j, 0:L], self.yT[:, j, 0:L], t[:, 0:L], ALU.mult)
        k.act(self.sqT[:, :, 0:L], self.yT[:, 0:8, 0:L], AF.Square)
        ng = self.CP["ssd_norm_g"]
        for g in range(2):
            pb = self.ps[6 + g]
            for j in range(4):
                k.mm(pb[:, 0:L], self.ones_b[:, :], self.sqT[:, g * 4 + j, 0:L], j == 0, j == 3)
            k.act(self.rstd[:, g, 0:L], pb[:, 0:L], AF.Sqrt, bias=EPS, scale=1.0 / 512)
            k.recip(self.rstd[:, g, 0:L], self.rstd[:, g, 0:L])
            for j in range(4):
                jj = g * 4 + j
                k.stt("dve", self.ynT[:, jj, 0:L], self.yT[:, jj, 0:L], self.colp[:, ng + jj:ng + jj + 1], self.rstd[:, g, 0:L], ALU.mult, ALU.mult)

    def ssd_chunk(self, l, sq, c, dslot, hst):
        k = self.k
        Q = sq.Q
        c0 = c * Q
        P4, P5, P6, P7 = self.ps[4], self.ps[5], self.ps[6], self.ps[7]
        dtb = self.rowp[0:Q, 0:16]; Ab = self.rowp[0:Q, 16:32]; Db = self.rowp
        pb = self.bank()
        for kk in range(KD):
            k.mm(pb[0:Q, 0:16], self.hT[:, kk, c0:c0 + Q], dslot.raw(kk * 16, [(1, 16)]), kk == 0, kk == KD - 1)
        k.tt("dve", self.s_dt[0:Q, :], pb[0:Q, 0:16], dtb, ALU.add)
        k.act(self.s_dt[0:Q, :], self.s_dt[0:Q, :], AF.Exp)
        k.act(self.s_dt[0:Q, :], self.s_dt[0:Q, :], AF.Ln, bias=1.0, scale=1.0)
        if self.sub(4): return
        k.tt("dve", self.s_dtA[0:Q, :], self.s_dt[0:Q, :], Ab, ALU.mult)
        k.tt("dve", self.s_r1.raw(0, [(64, 16), (1, Q)], 0, Q), self.ut.raw(0, [(0, 16), (1, Q)], 0, Q),
             self.s_dtA.raw(0, [(1, 16), (0, Q)], 0, Q), ALU.mult)
        for hb in range(2):
            pbk = (P4, P5)[hb]
            k.mm(pbk.raw(0, [(Q, 8), (1, Q)], 0, Q), self.lt[0:Q, 0:Q], self.s_r1.raw(hb * 8 * 64, [(64, 8), (1, Q)], 0, Q), True, True)
        k.mm(P6[0:Q, 0:16], self.lt[0:Q, 0:Q], self.s_dtA[0:Q, :], True, True, last=False)
        k.mm(P6[0:Q, 16:32], self.ut[0:Q, 0:Q], self.s_dtA[0:Q, :], True, True, last=False)
        k.mm(P6[:, 32:48], self.ones_f[0:Q, :], self.s_dtA[0:Q, :], True, True)
        if self.sub(5): return
        for hb in range(2):
            pbk = (P4, P5)[hb]
            k.act(self.s_E.raw(hb * 8 * 64, [(64, 8), (1, Q)], 0, Q), pbk.raw(0, [(Q, 8), (1, Q)], 0, Q), AF.Exp)
        if self.sub(5.3): return
        k.act(self.s_sm[0:Q, 0:32], P6[0:Q, 0:32], AF.Exp)
        if self.sub(5.6): return
        k.act(self.s_sm[:, 32:48], P6[:, 32:48], AF.Exp)
        if self.sub(6): return
        for hb in range(2):
            pbk = (P4, P5)[hb]
            for jj in range(4):
                j = hb * 4 + jj
                k.tr(pbk[0:Q, jj * 128:(jj + 1) * 128], self.xbcT[:, j, c0:c0 + Q], self.ident[:, :], last=(jj == 3))
            k.copy("act", self.s_xs[0:Q, hb * 512:(hb + 1) * 512], pbk[0:Q, 0:512])
        if self.sub(6.2): return
        for g in range(2):
            k.tr(P7[0:Q, g * 128:(g + 1) * 128], self.xbcT[:, 8 + g, c0:c0 + Q], self.ident[:, :], last=(g == 1))
        pcb = self.bank()
        for g in range(2):
            k.mm(pcb[0:Q, g * 64:g * 64 + Q], self.bcb[:, g, c0:c0 + Q], self.bcb[:, 2 + g, c0:c0 + Q], True, True, last=(g == 1))
        if self.sub(6.5): return
        k.copy("act", self.s_btm[0:Q, :], P7[0:Q, 0:256])
        if self.sub(6.7): return
        k.tt("dve", self.s_cbm.raw(0, [(64, 2), (1, Q)], 0, Q), pcb.raw(0, [(64, 2), (1, Q)], 0, Q),
             self.mk.raw(0, [(0, 2), (1, Q)], 0, Q), ALU.mult)
        if self.sub(7): return
        k.tt("dve", self.s_t.raw(0, [(64, 16), (1, 64)], 0, Q), self.s_xs.raw(0, [(64, 16), (1, 64)], 0, Q),
             self.s_dt.raw(0, [(1, 16), (0, 64)], 0, Q), ALU.mult)
        if self.sub(7.2): return
        k.copy("pool", self.s_xdt[0:Q, :], self.s_t[0:Q, :])
        if self.sub(7.4): return
        k.tt("dve", self.s_xdte.raw(0, [(64, 16), (1, 64)], 0, Q), self.s_t.raw(0, [(64, 16), (1, 64)], 0, Q),
             self.s_sm.raw(0, [(1, 16), (0, 64)], 0, Q), ALU.mult)
        if self.sub(7.6): return
        for g in range(2):
            k.tt("dve", self.s_MT.raw(g * 8 * 64, [(64, 8), (1, Q)], 0, Q), self.s_E.raw(g * 8 * 64, [(64, 8), (1, Q)], 0, Q),
                 self.s_cbm.raw(g * 64, [(0, 8), (1, Q)], 0, Q), ALU.mult)
        if self.sub(8): return
        for hb in range(2):
            pbk = (P4, P5)[hb]
            for hh in range(8):
                h = hb * 8 + hh
                k.mm(pbk[0:Q, hh * 64:(hh + 1) * 64], self.s_MT.raw(h * 64, [(1, Q)], 0, Q), self.s_xdt[0:Q, h * 64:(h + 1) * 64], True, True, last=(hh == 7))
        for g in range(2):
            pbk = (P6, P7)[g]
            k.mm(pbk[0:Q, 0:512], self.bcb[:, 2 + g, c0:c0 + Q], self.hstb[:, g * 512:(g + 1) * 512], True, True)
        if self.sub(9): return
        for g in range(2):
            po = (P6, P7)[g]; pd = (P4, P5)[g]
            ysl = self.s_y.raw(g * 512, [(64, 8), (1, 64)], 0, Q)
            k.tt("dve", ysl, po.raw(0, [(64, 8), (1, 64)], 0, Q), self.s_sm.raw(16 + g * 8, [(1, 8), (0, 64)], 0, Q), ALU.mult)
            k.tt("dve", self.s_y[0:Q, g * 512:(g + 1) * 512], self.s_y[0:Q, g * 512:(g + 1) * 512], pd[0:Q, 0:512], ALU.add)
        k.tt("pool", self.s_t.raw(0, [(64, 16), (1, 64)], 0, Q), self.s_xs.raw(0, [(64, 16), (1, 64)], 0, Q),
             Db.raw(32, [(1, 16), (0, 64)], 0, Q), ALU.mult)
        k.tt("pool", self.s_y[0:Q, :], self.s_y[0:Q, :], self.s_t[0:Q, :], ALU.add)
        if self.sub(10): return
        for g in range(2):
            pbk = (P4, P5)[g]
            k.mm(pbk[:, 0:512], self.s_btm[0:Q, g * 128:(g + 1) * 128], self.s_xdte[0:Q, g * 512:(g + 1) * 512], True, True)
        k.tt("dve", hst.raw(0, [(64, 16), (1, 64)]), hst.raw(0, [(64, 16), (1, 64)]), self.s_sm.raw(32, [(1, 16), (0, 64)]), ALU.mult)
        for g in range(2):
            pbk = (P4, P5)[g]
            k.tt("dve", hst[:, g * 512:(g + 1) * 512], hst[:, g * 512:(g + 1) * 512], pbk[:, 0:512], ALU.add)
        k.copy("act", self.hstb[:, :], hst[:, :])
        if self.sub(11): return
        for hb in range(2):
            pbk = (P6, P7)[hb]
            for jj in range(4):
                j = hb * 4 + jj
                k.tr(pbk[:, jj * 64:jj * 64 + Q], self.s_y[0:Q, j * 128:(j + 1) * 128], self.ident[0:Q, 0:Q], last=(jj == 3))
            k.copy("act", self.yT[:, hb * 4:(hb + 1) * 4, c0:c0 + Q], pbk.raw(0, [(64, 4), (1, Q)]))

    def attn_branch(self, l, sq, b):
        k = self.k
        L = sq.L
        t0 = b * L
        nsub = (L + 127) // 128
        for h in range(AH):
            slot = self.wslab("qkv", l, h)
            pb = self.proj(slot, 0, KD, self.hT, L, step=128)
            k.copy("act", self.qT[:, h, 0:L], pb[:, 0:L])
        for kv in range(2):
            tm = self.k_tm if kv == 0 else self.v_tm
            for h in range(AH):
                slot = self.wslab("qkv", l, 4 + kv * 4 + h)
                pb = self.proj(slot, 0, KD, self.hT, L, step=128)
                f = self.kvf[h % 2]
                k.copy("act", f[:, 0:L], pb[:, 0:L])
                if kv == 0:
                    k.copy("pool", self.kTb[:, h, 0:L], f[:, 0:L])
                for s in range(nsub):
                    n = min(128, L - s * 128)
                    pt = self.bank()
                    k.tr(pt[0:n, 0:128], f[:, s * 128:s * 128 + n], self.ident[:, :])
                    k.copy("dve", tm[0:n, s, h * 128:(h + 1) * 128], pt[0:n, 0:128])
                    if kv == 1:
                        k.copy("act", self.v_tmb[0:n, s, h, 0:128], pt[0:n, 0:128])
            for s in range(nsub):
                n = min(128, L - s * 128)
                if sq.sample:
                    dst = self.o_ks if kv == 0 else self.o_vs
                    r0 = (l * self.NS + (sq.idx - 1)) * SL + t0 + s * 128
                else:
                    dst = self.o_kp if kv == 0 else self.o_vp
                    r0 = l * self.SEQ + t0 + s * 128
                k.dma("pool", dst[r0:r0 + n, :], tm[0:n, s, :])
        if sq.sample:
            pass
        else:
            for h in range(AH):
                r0 = (l * AH + h) * 128
                k.dma("pool", self.Kh[r0:r0 + 128, t0:t0 + L], self.kTb[:, h, 0:L])
            for s in range(nsub):
                r0 = l * self.SEQ + t0 + s * 128
                k.dma("pool", self.Vh[r0:r0 + 128, :], self.v_tmb.raw(s * AH * 129, [(1, AH * 129)]))
        for h in range(AH):
            self.attn_head(l, sq, b, h)

    def attn_head(self, l, sq, b, h):
        k = self.k
        L = sq.L
        t0 = b * L
        nsub = (L + 127) // 128
        def acc(c, qs):
            i = c * nsub + qs
            return self.ps[4 + i], 0
        tiles = []
        if sq.sample:
            si = sq.idx - 1
            for t in range(8):
                r0 = (l * self.NS + si) * PAST + t * 128
                stg = self.xstage[t % 2]
                k.dma("sp", stg[:, 0:128], self.ck[r0:r0 + 128, h * 128:(h + 1) * 128])
                pt = self.bank()
                k.tr(pt[:, 0:128], stg[:, 0:128], self.ident[:, :])
                k.copy("act" if t % 2 else "dve", self.KsT[:, t * 128:(t + 1) * 128], pt[:, 0:128])
                k.dma("sp", stg[:, 128:256], self.cv[r0:r0 + 128, h * 128:(h + 1) * 128])
                k.copy("pool", self.Vs[:, t, 0:128], stg[:, 128:256])
            k.copy("pool", self.KsT[:, PAST:PAST + SL], self.kTb[:, h, 0:SL])
            k.copy("pool", self.Vs[0:SL, 8, :], self.v_tmb[0:SL, 0, h, :])
            for t in range(8):
                tiles.append(("s", t, 128, t - 8, None))
            tiles.append(("s", 8, SL, 0, 0))
            ngroups = 1
        else:
            ntile = (t0 + L) // 128
            for jt in range(ntile):
                if t0 - (128 * jt + 127) > WIN[h]:
                    continue
                off = jt - t0 // 128
                tiles.append(("p", jt, 128, off, off if off >= 0 else None))
        first = {}
        gi_loaded = {}
        lastfor = {}
        for ti, (kind, jt, nk, off, diag) in enumerate(tiles):
            ql = 0 if diag is None else diag
            for qs in range(ql, nsub):
                lastfor[qs] = ti
        for ti, (kind, jt, nk, off, diag) in enumerate(tiles):
            if kind == "p":
                g = jt // 4
                if g not in gi_loaded:
                    kg = self.Kg[g % 2]; vg = self.Vg[g % 2]
                    nt = min(4, (t0 + L) // 128 - g * 4)
                    r0 = (l * AH + h) * 128
                    k.dma("sp", kg[:, 0:nt * 128], self.Kh[r0:r0 + 128, g * 512:g * 512 + nt * 128])
                    vsrc = self.Vh.dap(bass.AP(self.Vh.t, (l * self.SEQ + g * 512) * AH * 129 + h * 129,
                                               [[AH * 129, 128], [128 * AH * 129, nt], [1, 129]]))
                    k.dma("sp", vg[:, 0:nt, :], vsrc)
                    gi_loaded[g] = (kg, vg)
                kg, vg = gi_loaded[g]
                jj = jt % 4
                Kv = lambda c, kg=kg, jj=jj: kg[c * 64:(c + 1) * 64, jj * 128:jj * 128 + 128]
                Vv = lambda vg=vg, jj=jj: vg[:, jj, :]
            else:
                Kv = lambda c, jt=jt, nk=nk: self.KsT[c * 64:(c + 1) * 64, jt * 128:jt * 128 + nk]
                Vv = lambda jt=jt, nk=nk: self.Vs[0:nk, jt, :]
            qlo = 0 if diag is None else diag * 128
            ncol = L - qlo
            for c in range(2):
                pb = self.bank()
                k.mm(pb[0:nk, 0:ncol], Kv(c), self.qT[c * 64:(c + 1) * 64, h, qlo:L], True, True)
                pt = self.PT[(ti * 2 + c) % 3]
                k.act(pt[0:nk, 0:ncol], pb[0:nk, 0:ncol], AF.Exp, bias=self.alb[0:nk, h * NOFF + off + 127:h * NOFF + off + 128], scale=0.125)
                if diag is not None:
                    n = min(128, ncol)
                    k.tt("pool", pt[0:nk, 0:n], pt[0:nk, 0:n], self.cm[0:nk, h * 128:h * 128 + n], ALU.mult)
                for qs in range(qlo // 128, nsub):
                    n = min(128, L - qs * 128)
                    bk, co = acc(c, qs)
                    st_ = (c, qs) not in first
                    first[(c, qs)] = True
                    lastt = (ti == lastfor[qs])
                    k.mm(bk[0:n, co:co + 129], pt[0:nk, qs * 128 - qlo:qs * 128 - qlo + n], Vv(), st_, lastt, last=(qs == nsub - 1))
        for qs in range(nsub):
            n = min(128, L - qs * 128)
            b0, c0_ = acc(0, qs); b1, c1_ = acc(1, qs)
            k.recip(self.osm[0:n, 0:1], b0[0:n, c0_ + 128:c0_ + 129])
            k.recip(self.osm[0:n, 1:2], b1[0:n, c1_ + 128:c1_ + 129])
            k.tt("dve", self.osm[0:n, 2:3], self.osm[0:n, 1:2], self.lamc[0:n, 0:1], ALU.mult)
            k.ts("dve", self.o0[0:n, :], b0[0:n, c0_:c0_ + 128], self.osm[0:n, 0:1], None, ALU.mult)
            k.stt("dve", self.o1[0:n, :], b1[0:n, c1_:c1_ + 128], self.osm[0:n, 2:3], self.o0[0:n, :], ALU.mult, ALU.add)
            k.act(self.osq[0:n, :], self.o1[0:n, :], AF.Square, accum=self.osm[0:n, 3:4])
            k.act(self.osm[0:n, 4:5], self.osm[0:n, 3:4], AF.Sqrt, bias=1e-5, scale=1.0 / 128)
            k.recip(self.osm[0:n, 5:6], self.osm[0:n, 4:5])
            k.stt("dve", self.o0[0:n, :], self.o1[0:n, :], self.osm[0:n, 5:6], self.rowp[0:n, 48:176], ALU.mult, ALU.mult)
            pt = self.bank()
            k.tr(pt[:, 0:n], self.o0[0:n, :], self.ident[0:n, 0:n])
            k.copy("act", self.oT[:, h, qs * 128:qs * 128 + n], pt[:, 0:n])

    def merge(self, l, sq, L):
        k = self.k
        bg = self.CP["b_gate"]
        for m in range(KD):
            gs = self.wslab("gate", l, m)
            bs = self.wslab("br", l, m)
            pgs = [self.proj(gs, i * 128, KD, self.hT, L, step=384) for i in range(3)]
            for i in range(3):
                k.act(self.tmp3[:, i, 0:L], pgs[i][:, 0:L], AF.Sigmoid, bias=self.colp[:, bg + i * 8 + m:bg + i * 8 + m + 1], scale=1.0)
            srcs = [(self.cvT, 4, 0), (self.ynT, 8, 4), (self.oT, 4, 12)]
            for i, (rb, K, kc0) in enumerate(srcs):
                pb = self.bank()
                for kk in range(K):
                    k.mm(pb[:, 0:L], bs.raw((kc0 + kk) * 128, [(1, 128)]), rb[:, kk, 0:L], kk == 0, kk == K - 1)
                t = self.tmpm[i]
                k.tt("dve", t[:, 0:L], self.tmp3[:, i, 0:L], pb[:, 0:L], ALU.mult)
            k.tt("pool", self.tmpm[0][:, 0:L], self.tmpm[0][:, 0:L], self.tmpm[1][:, 0:L], ALU.add)
            k.tt("pool", self.mergedT[:, m, 0:L], self.tmpm[0][:, 0:L], self.tmpm[2][:, 0:L], ALU.add)
        for m in range(KD):
            slot = self.wslab("mix", l, m)
            pb = self.proj(slot, 0, KD, self.mergedT, L, step=128)
            k.copy("act" if m % 2 else "dve", self.fT[:, m, 0:L], pb[:, 0:L])


def host_consts():
    ident = np.eye(128, dtype=np.float32)
    i = np.arange(64)
    lt = (i[:, None] > i[None, :]).astype(np.float32)
    ut = (i[:, None] <= i[None, :]).astype(np.float32)
    mk = (i[:, None] <= i[None, :]).astype(np.float32)
    p = np.arange(128)
    alb = np.zeros((128, AH, NOFF), np.float32)
    for h in range(AH):
        for o in range(NOFF):
            alb[:, h, o] = SLOPES[h] * (128.0 * (o - 127) + p)
    cm = np.zeros((128, AH, 128), np.float32)
    kk = p[:, None]; qq = p[None, :]
    for h in range(AH):
        vis = (kk // 64) <= (qq // 64)
        fut = kk > qq
        cm[:, h, :] = np.where(vis, np.where(fut, np.exp(-2.0 * SLOPES[h] * (kk - qq)), 1.0), 0.0)
    pm = np.stack([np.ones(128, np.float32), -np.ones(128, np.float32)])
    return dict(c_ident=ident, c_lt=lt, c_ut=ut, c_mk=mk, c_alb=alb.reshape(128, AH * NOFF),
                c_cm=cm.reshape(128, AH * 128), c_pm=pm)


def make_in_maps(inp, SEQ, NS, DEPTH, ncores=8):
    f = lambda a: np.ascontiguousarray(np.asarray(a, dtype=np.float32))
    shared = dict(
        w_ada=f(inp["w_ada"]).reshape(DEPTH * D, 9216), b_ada=f(inp["b_ada"]).reshape(DEPTH * 72, 128),
        norm_pre=f(inp["norm_pre"]).reshape(DEPTH * 24, 128), norm_post=f(inp["norm_post"]).reshape(DEPTH * 24, 128),
        w_ffn_in=f(inp["w_ffn_in"]).reshape(DEPTH * 2 * D, 2 * FF), w_ffn_out=f(inp["w_ffn_out"]).reshape(DEPTH * 2 * FF, D),
        w_in=f(inp["w_in"]).reshape(DEPTH * D, NIN), b_gate=f(inp["b_gate"]).reshape(DEPTH * 24, 128),
        conv_dw_w=f(inp["conv_dw_w"]).reshape(DEPTH * 124, 128), conv_dw_b=f(inp["conv_dw_b"]).reshape(DEPTH * 4, 128),
        conv_ln_g=f(inp["conv_ln_g"]).reshape(DEPTH * 4, 128), conv_ln_b=f(inp["conv_ln_b"]).reshape(DEPTH * 4, 128),
        w_br_conv=f(inp["w_br_conv"]).reshape(DEPTH * 512, D),
        ssd_conv_w=f(inp["ssd_conv_w"]).reshape(DEPTH * 48, 128), ssd_conv_b=f(inp["ssd_conv_b"]).reshape(DEPTH * 12, 128),
        ssd_dt_bias=f(inp["ssd_dt_bias"]).reshape(DEPTH, 16), ssd_A_log=f(inp["ssd_A_log"]).reshape(DEPTH, 16),
        ssd_D=f(inp["ssd_D"]).reshape(DEPTH, 16), ssd_norm_g=f(inp["ssd_norm_g"]).reshape(DEPTH * 8, 128),
        w_br_ssd=f(inp["w_br_ssd"]).reshape(DEPTH * D, D),
        lambda_q=f(inp["lambda_q"]).reshape(DEPTH * 2, 64), lambda_k=f(inp["lambda_k"]).reshape(DEPTH * 2, 64),
        attn_subln_g=f(inp["attn_subln_g"]).reshape(DEPTH, 128),
        w_br_attn=f(inp["w_br_attn"]).reshape(DEPTH * 512, D), w_mix_out=f(inp["w_mix_out"]).reshape(DEPTH * D, D),
    )
    shared.update(host_consts())
    xp, cp = f(inp["x_prompt"]), f(inp["c_prompt"])
    xs, cs = f(inp["x_sample"]), f(inp["c_sample"])
    ck, cv = f(inp["cache_attn_k"]), f(inp["cache_attn_v"])
    sc, ssc, ssd = f(inp["state_conv"]), f(inp["state_ssd_conv"]), f(inp["state_ssd"])
    nb = xp.shape[0]
    maps = []
    for c in range(ncores):
        m = dict(shared)
        if c < nb:
            m["xp"] = xp[c][:SEQ]; m["cp"] = cp[c][None, :]
        else:
            m["xp"] = np.zeros((SEQ, D), np.float32); m["cp"] = np.zeros((1, D), np.float32)
        sl = slice(c * NS, (c + 1) * NS)
        m["xs"] = xs[sl].reshape(NS * SL, D); m["cs"] = cs[sl]
        m["ck"] = ck[:, sl].reshape(DEPTH * NS * PAST, 512); m["cv"] = cv[:, sl].reshape(DEPTH * NS * PAST, 512)
        m["sconv"] = sc[:, sl].reshape(DEPTH * NS * 30, 512); m["sssdc"] = ssc[:, sl].reshape(DEPTH * NS * 3, XBC)
        m["sssd"] = ssd[:, sl].reshape(DEPTH * NS * 1024, 128)
        maps.append({k_: np.ascontiguousarray(v) for k_, v in m.items()})
    return maps


_CACHE = {}


def run(inp, SEQ, NS=2, DEPTH=2, BL=256, ncores=8):
    key = (SEQ, NS, DEPTH, BL)
    if key not in _CACHE:
        _CACHE[key] = Kern(SEQ, NS, DEPTH, BL).build()
    nc = _CACHE[key]
    maps = make_in_maps(inp, SEQ, NS, DEPTH, ncores)
    res = run_bass_kernel_spmd(nc, maps, core_ids=list(range(ncores)))
    R = res.results
    nb = min(ncores, np.asarray(inp["x_prompt"]).shape[0])
    g = lambda name, c: np.asarray(R[c][name], dtype=np.float32)
    y_p = np.stack([g("o_yp", c) for c in range(nb)])
    y_s = np.concatenate([g("o_ys", c).reshape(NS, SL, D) for c in range(ncores)], 0)
    k_p = np.stack([g("o_kp", c).reshape(DEPTH, SEQ, AH, 128) for c in range(nb)], 1)
    v_p = np.stack([g("o_vp", c).reshape(DEPTH, SEQ, AH, 128) for c in range(nb)], 1)
    conv_p = np.stack([g("o_convp", c).reshape(DEPTH, 30, 512) for c in range(nb)], 1)
    ssdc_p = np.stack([g("o_ssdcp", c).reshape(DEPTH, 3, XBC) for c in range(nb)], 1)
    ssd_p = np.stack([g("o_ssdp", c).reshape(DEPTH, SH, SP, SN) for c in range(nb)], 1)
    k_s = np.concatenate([g("o_ks", c).reshape(DEPTH, NS, SL, AH, 128) for c in range(ncores)], 1)
    v_s = np.concatenate([g("o_vs", c).reshape(DEPTH, NS, SL, AH, 128) for c in range(ncores)], 1)
    conv_s = np.concatenate([g("o_convs", c).reshape(DEPTH, NS, 30, 512) for c in range(ncores)], 1)
    ssdc_s = np.concatenate([g("o_ssdcs", c).reshape(DEPTH, NS, 3, XBC) for c in range(ncores)], 1)
    ssd_s = np.concatenate([g("o_ssds", c).reshape(DEPTH, NS, SH, SP, SN) for c in range(ncores)], 1)
    return (y_p, y_s, k_p, v_p, conv_p, ssdc_p, ssd_p, k_s, v_s, conv_s, ssdc_s, ssd_s)


def kernel(**inputs):
    return run(inputs, SEQ=16384, NS=2, DEPTH=2, BL=256, ncores=8)
```
